# Optimizing a Trainium2 kernel written in Bass

```python
import jax, jax.numpy as jnp
from jax import lax
import numpy as np

D_MODEL = 2048
BATCH = 8
SEQ = 4096
DEPTH = 4

MEM_TOKENS = 256
EPS = 1e-6
ROPE_THETA = 10000.0
Q_BLOCK = 128
MLA_HEADS = D_MODEL // 256
QK_NOPE_DIM = 128
QK_ROPE_DIM = 64
QK_HEAD_DIM = QK_NOPE_DIM + QK_ROPE_DIM
V_HEAD_DIM = 128
Q_LORA_RANK = D_MODEL // 4
KV_LORA_RANK = D_MODEL // 8
MLA_WIDTH = MLA_HEADS * V_HEAD_DIM
CONV_WIDTH = D_MODEL // 4
CONV_K = 3
MEM_HEADS = 4
MEM_HEAD_DIM = D_MODEL // 16
MEM_WIDTH = MEM_HEADS * MEM_HEAD_DIM
MIX_WIDTH = MLA_WIDTH + CONV_WIDTH + MEM_WIDTH
IN_SPLITS = (Q_LORA_RANK, KV_LORA_RANK, QK_ROPE_DIM, CONV_WIDTH, CONV_WIDTH, CONV_WIDTH, MEM_WIDTH, MIX_WIDTH)
IN_COLS = Q_LORA_RANK + KV_LORA_RANK + QK_ROPE_DIM + 3 * CONV_WIDTH + MEM_WIDTH + MIX_WIDTH

kernel_name = "hybrid_mla_shortconv_memory_encoder"


def rmsnorm(x, g):
    x32 = x.astype(jnp.float32)
    y = x32 * lax.rsqrt(jnp.mean(x32 * x32, axis=-1, keepdims=True) + EPS)
    return (y * g.astype(jnp.float32)).astype(x.dtype)


def rope_tables(positions):
    inv_freq = 1.0 / (ROPE_THETA ** (jnp.arange(0, QK_ROPE_DIM, 2, dtype=jnp.float32) / QK_ROPE_DIM))
    ang = positions.astype(jnp.float32)[..., None] * inv_freq
    return jnp.cos(ang), jnp.sin(ang)


def apply_rope(x, cos, sin):
    half = x.shape[-1] // 2
    x32 = x.astype(jnp.float32)
    x1, x2 = x32[..., :half], x32[..., half:]
    return jnp.concatenate([x1 * cos - x2 * sin, x2 * cos + x1 * sin], axis=-1).astype(x.dtype)


def split_cols(z):
    idx = list(np.cumsum(IN_SPLITS)[:-1])
    return jnp.split(z, idx, axis=-1)


def mla_attention(q, k, v):
    b, s, h, dq = q.shape
    nb = s // Q_BLOCK
    scale = QK_HEAD_DIM ** -0.5
    qb = q.reshape(b, nb, Q_BLOCK, h, dq).transpose(1, 0, 2, 3, 4)

    def block(qi):
        sc = jnp.einsum('bqhd,bkhd->bhqk', qi, k).astype(jnp.float32) * scale
        p = jax.nn.softmax(sc, axis=-1).astype(v.dtype)
        return jnp.einsum('bhqk,bkhd->bqhd', p, v)

    o = lax.map(block, qb)
    return o.transpose(1, 0, 2, 3, 4).reshape(b, s, h * V_HEAD_DIM)


def short_gated_conv(gb, gc, xin, w):
    u = gc * xin
    up = jnp.pad(u, ((0, 0), (1, 1), (0, 0)))
    conv = up[:, :-2] * w[0] + up[:, 1:-1] * w[1] + up[:, 2:] * w[2]
    return gb * conv


def memory_attention(q, mem_n, w_mk, w_mv):
    b, m, _ = mem_n.shape
    mk = (mem_n @ w_mk).reshape(b, m, MEM_HEADS, MEM_HEAD_DIM)
    mv = (mem_n @ w_mv).reshape(b, m, MEM_HEADS, MEM_HEAD_DIM)
    sc = jnp.einsum('bshd,bmhd->bhsm', q, mk).astype(jnp.float32) * (MEM_HEAD_DIM ** -0.5)
    p = jax.nn.softmax(sc, axis=-1).astype(mv.dtype)
    o = jnp.einsum('bhsm,bmhd->bshd', p, mv)
    return o.reshape(b, q.shape[1], MEM_WIDTH)


def setup_inputs(seed: int = 0) -> dict:
    key = jax.random.key(seed)
    ks = jax.random.split(key, 16)
    f32 = jnp.float32

    def w(k, shape, fan_in):
        return jax.random.normal(k, shape, f32) * (fan_in ** -0.5)

    def gain(k, shape):
        return 1.0 + 0.02 * jax.random.normal(k, shape, f32)

    x = jax.random.normal(ks[0], (BATCH, SEQ, D_MODEL), f32)
    mem = jax.random.normal(ks[1], (BATCH, MEM_TOKENS, D_MODEL), f32)
    offset = jax.random.randint(ks[2], (BATCH, 1), 0, 4096, dtype=jnp.int32)
    positions = (offset + jnp.arange(SEQ, dtype=jnp.int32)[None, :]).astype(jnp.int32)
    return {
        "x": x,
        "mem": mem,
        "positions": positions,
        "pre_norm_g": gain(ks[3], (DEPTH, D_MODEL)),
        "w_in": w(ks[4], (DEPTH, D_MODEL, IN_COLS), D_MODEL),
        "q_norm_g": gain(ks[5], (DEPTH, Q_LORA_RANK)),
        "w_uq": w(ks[6], (DEPTH, Q_LORA_RANK, MLA_HEADS * QK_HEAD_DIM), Q_LORA_RANK),
        "kv_norm_g": gain(ks[7], (DEPTH, KV_LORA_RANK)),
        "w_ukv": w(ks[8], (DEPTH, KV_LORA_RANK, MLA_HEADS * (QK_NOPE_DIM + V_HEAD_DIM)), KV_LORA_RANK),
        "conv_w": w(ks[9], (DEPTH, CONV_K, CONV_WIDTH), CONV_K),
        "mem_norm_g": gain(ks[10], (DEPTH, D_MODEL)),
        "w_mk": w(ks[11], (DEPTH, D_MODEL, MEM_WIDTH), D_MODEL),
        "w_mv": w(ks[12], (DEPTH, D_MODEL, MEM_WIDTH), D_MODEL),
        "w_o": w(ks[13], (DEPTH, MIX_WIDTH, D_MODEL), MIX_WIDTH),
        "post_norm_g": gain(ks[14], (DEPTH, D_MODEL)),
    }


def reference(x, mem, positions, pre_norm_g, w_in, q_norm_g, w_uq, kv_norm_g, w_ukv, conv_w,
              mem_norm_g, w_mk, w_mv, w_o, post_norm_g):
    b, s, _ = x.shape
    cos, sin = rope_tables(positions)
    for l in range(DEPTH):
        h = rmsnorm(x, pre_norm_g[l])
        z = h @ w_in[l]
        q_lat, kv_lat, k_pe, gb, gc, xin, q_mem, gate = split_cols(z)

        q = (rmsnorm(q_lat, q_norm_g[l]) @ w_uq[l]).reshape(b, s, MLA_HEADS, QK_HEAD_DIM)
        q = jnp.concatenate([q[..., :QK_NOPE_DIM],
                             apply_rope(q[..., QK_NOPE_DIM:], cos[:, :, None, :], sin[:, :, None, :])], axis=-1)
        kv = (rmsnorm(kv_lat, kv_norm_g[l]) @ w_ukv[l]).reshape(b, s, MLA_HEADS, QK_NOPE_DIM + V_HEAD_DIM)
        k_nope, v = kv[..., :QK_NOPE_DIM], kv[..., QK_NOPE_DIM:]
        k_pe = apply_rope(k_pe, cos, sin)
        k = jnp.concatenate([k_nope, jnp.broadcast_to(k_pe[:, :, None, :], (b, s, MLA_HEADS, QK_ROPE_DIM))], axis=-1)
        a_out = mla_attention(q, k, v)

        c_out = short_gated_conv(gb, gc, xin, conv_w[l])

        mem_n = rmsnorm(mem, mem_norm_g[l])
        m_out = memory_attention(q_mem.reshape(b, s, MEM_HEADS, MEM_HEAD_DIM), mem_n, w_mk[l], w_mv[l])

        y = jnp.concatenate([a_out, c_out, m_out], axis=-1) * jax.nn.silu(gate)
        o = y @ w_o[l]
        x = x + rmsnorm(o, post_norm_g[l])
    return x
```

```python
import math
from contextlib import ExitStack

import numpy as np
import concourse.bass as bass
import concourse.mybir as mybir
from concourse.bass_utils import run_bass_kernel_spmd

F32 = mybir.dt.float32
BF16 = mybir.dt.bfloat16
I32 = mybir.dt.int32
AF = mybir.ActivationFunctionType
ALU = mybir.AluOpType

S = 4096
D = 2048
DEPTH = 4
INC = 4928
EPS = 1e-6
NCORES = 8
NEWCTR = False
FUSED = False
WL0 = False
SRCX = False
HEADS = 8
CQ, CKV, CKPE, CGB, CGC, CXIN, CQM, CG = 0, 512, 768, 832, 1344, 1856, 2368, 2880
TWO_PI = 2.0 * math.pi
CW1 = 6.28125
CW2 = TWO_PI - CW1
PI_LO = 3.1415925


class Ctr:
    __slots__ = ("sem", "step", "count")

    def __init__(self, sem, step):
        self.sem = sem
        self.step = step
        self.count = 0


def _flat(ws, out):
    for w in ws:
        if w is None:
            continue
        if isinstance(w, tuple) and len(w) == 2 and isinstance(w[0], Ctr):
            out.append(w)
        else:
            _flat(w, out)
    return out


class Prog:
    ENGS = ("pe", "act", "dve", "pool", "sp")

    def __init__(self, nc, es):
        self.nc = nc
        self.es = es
        self.q = {e: [] for e in self.ENGS}
        self.pending = {e: [] for e in self.ENGS}
        self.all_ctrs = []
        self.free_dma = []
        self.used_dma = []
        self.ectr = {}
        self.nsem = 0
        self.new_engine_counters()

    def _sem(self, name):
        self.nsem += 1
        return self.es.enter_context(self.nc.semaphore(name))

    def new_engine_counters(self):
        for e in ("pe", "act", "dve", "pool"):
            c = Ctr(self._sem(f"c_{e}_{self.nsem}"), 1)
            self.ectr[e] = c
            self.all_ctrs.append(c)

    def newctr(self):
        if self.free_dma:
            c = self.free_dma.pop()
        else:
            c = Ctr(self._sem(f"d_{self.nsem}"), 16)
            self.all_ctrs.append(c)
        self.used_dma.append(c)
        return c

    def op(self, eng, fn, waits=(), inc=True):
        ws = _flat(waits, []) + self.pending[eng]
        self.pending[eng] = []
        c = self.ectr[eng] if inc else None
        tok = None
        if c is not None:
            c.count += 1
            tok = (c, c.count)
        self.q[eng].append((ws, fn, c))
        return tok

    def dma(self, eng, out, in_, ctr, waits=(), slow=False):
        ws = _flat(waits, []) + self.pending[eng]
        self.pending[eng] = []
        ctr.count += 16
        if slow:
            fn = lambda e: e.dma_start(out=out, in_=in_, allow_slow_non_contiguous=True)
        else:
            fn = lambda e: e.dma_start(out=out, in_=in_)
        self.q[eng].append((ws, fn, ctr))
        return (ctr, ctr.count)

    def barrier(self):
        toks = [(c, c.count) for c in self.all_ctrs if c.count > 0]
        for e in self.ENGS:
            self.pending[e] = self.pending[e] + list(toks)
        self.free_dma.extend(self.used_dma)
        self.used_dma = []

    def emit(self, eng, e):
        seen = {}
        for ws, fn, c in self.q[eng]:
            for (ctr, val) in ws:
                if seen.get(id(ctr), 0) >= val:
                    continue
                e.wait_ge(ctr.sem, val)
                seen[id(ctr)] = val
            ins = fn(e)
            if c is not None:
                ins.then_inc(c.sem, c.step)
        for (ctr, val) in self.pending[eng]:
            if seen.get(id(ctr), 0) >= val:
                continue
            e.wait_ge(ctr.sem, val)
            seen[id(ctr)] = val


class Arena:
    def __init__(self, nc, es, nbytes):
        self.cap = nbytes
        self.t = es.enter_context(nc.sbuf_tensor("arena", [128, nbytes // 2], BF16))
        self.off = 0

    def mark(self):
        return self.off

    def reset(self, m):
        self.off = m

    def alloc(self, shape, dtype):
        esz = 2 if dtype == BF16 else 4
        n = 1
        for s_ in shape:
            n *= s_
        self.off = (self.off + 63) // 64 * 64
        nb = n * esz
        assert self.off + nb <= self.cap, f"arena overflow {self.off + nb} > {self.cap}"
        a = self.t[:, self.off // 2:(self.off + nb) // 2]
        self.off += nb
        if dtype != BF16:
            a = a.bitcast(dtype)
        if len(shape) == 2:
            a = a.rearrange("p (a b) -> p a b", b=shape[1])
        elif len(shape) == 3:
            a = a.rearrange("p (a b c) -> p a b c", b=shape[1], c=shape[2])
        return a


class Ring:
    def __init__(self, prog, bufs, ctr=True):
        self.bufs = bufs
        self.n = len(bufs)
        self.ctr = [prog.newctr() for _ in bufs] if ctr else None
        self.free = [[] for _ in bufs]
        self.i = 0

    def take(self):
        idx = self.i % self.n
        self.i += 1
        return idx


class TmpPool:
    def __init__(self, prog, bufs):
        self.bufs = bufs
        self.n = len(bufs)
        self.ctr = [prog.newctr() for _ in bufs]
        self.free = [[] for _ in bufs]
        self.i = 0

    def get(self):
        idx = self.i % self.n
        self.i += 1
        w = self.free[idx]
        self.free[idx] = []
        return idx, w

    def f32(self, idx):
        return self.bufs[idx]

    def bf16(self, idx):
        return self.bufs[idx].bitcast(BF16)[:, 0:512]

    def used(self, idx, tok):
        if tok is not None:
            self.free[idx].append(tok)


def MM(out, lhsT, rhs, start, stop):
    return lambda e: e.matmul(out, lhsT, rhs, start=start, stop=stop)


def TR(out, in_, ident):
    return lambda e: e.transpose(out, in_, ident)


def ACTV(out, in_, func, **kw):
    return lambda e: e.activation(out, in_, func, **kw)


def TT(out, a, b, op):
    return lambda e: e.tensor_tensor(out, a, b, op)


def TS(out, a, s1, s2, op0, op1=None):
    if op1 is None:
        return lambda e: e.tensor_scalar(out, a, s1, s2, op0)
    return lambda e: e.tensor_scalar(out, a, s1, s2, op0, op1)


def STT(out, in0, scalar, in1, op0, op1):
    return lambda e: e.scalar_tensor_tensor(out, in0, scalar, in1, op0, op1)


def CP(out, in_):
    def f(e):
        if hasattr(e, "tensor_copy"):
            return e.tensor_copy(out, in_)
        return e.activation(out, in_, AF.Copy)
    return f


def RCP(out, in_):
    return lambda e: e.reciprocal(out, in_)


def MSET(ap, c):
    return lambda e: e.memset(ap, c)


def build_program(nl=DEPTH, dbg=False, wdepth=DEPTH):
    nc = bass.Bass("TRN2", target_bir_lowering=False)

    def din(name, shape, dt=F32):
        return nc.dram_tensor(name, list(shape), dt, kind="ExternalInput").ap()

    x_d = din("x", [S, D])
    mem_d = din("mem", [256, D])
    pos_d = din("pos", [1, S], I32)
    rc_d = din("ropec", [128, 2])
    pre_g = din("pre_norm_g", [wdepth, D])
    w_in = din("w_in", [wdepth, D, INC])
    q_g = din("q_norm_g", [wdepth, 512])
    w_uq = din("w_uq", [wdepth, 512, 1536])
    kv_g = din("kv_norm_g", [wdepth, 256])
    w_ukv = din("w_ukv", [wdepth, 256, 2048])
    conv_w = din("conv_w", [wdepth, 3, 512])
    mem_g = din("mem_norm_g", [wdepth, D])
    w_mk = din("w_mk", [wdepth, D, 512])
    w_mv = din("w_mv", [wdepth, D, 512])
    w_o = din("w_o", [wdepth, D, D])
    post_g = din("post_norm_g", [wdepth, D])
    out_d = nc.dram_tensor("out", [S, D], F32, kind="ExternalOutput").ap()

    skind = "ExternalOutput" if dbg else "Internal"

    def dscr(name, shape, dt):
        return nc.dram_tensor(name, list(shape), dt, kind=skind).ap()

    QN = dscr("QN", [512, S], BF16)
    KVN = dscr("KVN", [256, S], BF16)
    KPE = dscr("KPE", [128, S], BF16)
    U = dscr("U", [512, S], F32)
    GBSG = dscr("GBSG", [512, S], BF16)
    SG = dscr("SG", [1024, S], BF16)
    YT = dscr("YT", [2048, S], BF16)
    XS = nc.dram_tensor("XS", [S, D], F32, kind="Internal").ap()
    QNv = QN.rearrange("(c p) t -> p c t", p=128)
    KVNv = KVN.rearrange("(c p) t -> p c t", p=128)
    Uv = U.rearrange("(c p) t -> p c t", p=128)
    GBSGv = GBSG.rearrange("(c p) t -> p c t", p=128)
    SGv = SG.rearrange("(c p) t -> p c t", p=128)
    YTv = YT.rearrange("(c p) t -> p c t", p=128)

    es = ExitStack()
    with es:
        P = Prog(nc, es)
        arena = Arena(nc, es, 206 * 1024)
        PS = [es.enter_context(nc.psum_tensor(f"ps{i}", [128, 512], F32)) for i in range(8)]
        PSA = [p[:] for p in PS]
        PSB = [p[:].bitcast(BF16).rearrange("p (a b) -> p a b", b=128) for p in PS]
        bankfree = [[] for _ in range(8)]

        ident = arena.alloc([128], BF16)
        ones = arena.alloc([128], BF16)
        dmat = arena.alloc([128], BF16)
        TQ = arena.alloc([S], F32)
        rc = arena.alloc([2], F32)
        gq = arena.alloc([DEPTH, 4], F32)
        gkv = arena.alloc([DEPTH, 2], F32)
        cw = arena.alloc([DEPTH, 3, 4], F32)
        MKT = arena.alloc([DEPTH, 4, 256], BF16)
        MV = arena.alloc([DEPTH, 2, 512], BF16)
        persist_mark = arena.mark()

        def phase0():
            iot = arena.alloc([128], F32)
            ta = arena.alloc([128], F32)
            tb = arena.alloc([128], F32)
            tcc = arena.alloc([128], F32)
            t_iota = P.op("pool", lambda e: e.iota(iot, [[1, 128]], base=0, channel_multiplier=-1,
                                                   allow_small_or_imprecise_dtypes=True))
            t1 = P.op("dve", TS(ta, iot, 0.0, None, ALU.is_equal), [t_iota])
            t2 = P.op("dve", CP(ident, ta), [t1])
            t3 = P.op("dve", TS(tb, iot, 64.0, None, ALU.is_equal), [t_iota])
            t4 = P.op("dve", TS(tcc, iot, -64.0, None, ALU.is_equal), [t_iota])
            t5 = P.op("dve", TT(ta, ta, tb, ALU.add), [t1, t2, t3])
            t6 = P.op("dve", TT(ta, ta, tcc, ALU.add), [t5, t4])
            P.op("dve", CP(dmat, ta), [t6])
            P.op("dve", MSET(ones, 1.0))

            c0 = P.newctr()
            for l in range(nl):
                P.dma("sp", gq[:, l, :], q_g[l].rearrange("(c p) -> p c", p=128), c0, slow=True)
                P.dma("sp", gkv[:, l, :], kv_g[l].rearrange("(c p) -> p c", p=128), c0, slow=True)
                for k in range(3):
                    P.dma("sp", cw[:, l, k, :], conv_w[l, k].rearrange("(c p) -> p c", p=128), c0, slow=True)
            P.dma("sp", rc, rc_d, c0)
            t_c0 = (c0, c0.count)

            posi = arena.alloc([S], I32)
            ang = arena.alloc([S], F32)
            kf = arena.alloc([S], F32)
            ki = arena.alloc([S], I32)
            c1 = P.newctr()
            td = P.dma("sp", posi, pos_d.broadcast_to([128, S]), c1)
            a = P.op("dve", CP(ang, posi), [td])
            a = P.op("dve", TS(ang, ang, rc[:, 0:1], None, ALU.mult), [a, t_c0])
            for rnd in range(2):
                a = P.op("dve", TS(kf, ang, 1.0 / TWO_PI, None, ALU.mult), [a])
                a = P.op("dve", CP(ki, kf), [a])
                a = P.op("dve", CP(kf, ki), [a])
                a = P.op("dve", STT(ang, kf, -CW1, ang, ALU.mult, ALU.add), [a])
                a = P.op("dve", STT(ang, kf, -CW2, ang, ALU.mult, ALU.add), [a])
                if rnd == 0:
                    a = P.op("dve", TS(ang, ang, rc[:, 1:2], None, ALU.add), [a])
            a = P.op("dve", TS(ang, ang, PI_LO, -PI_LO, ALU.min, ALU.max), [a])
            P.op("act", ACTV(TQ, ang, AF.Sin), [a])
            P.barrier()
            arena.reset(persist_mark)

            memx = arena.alloc([2, D], F32)
            junk = arena.alloc([D], BF16)
            st = arena.alloc([8], F32)
            gm = arena.alloc([D], F32)
            memn = arena.alloc([2, D], BF16)
            memnT = arena.alloc([16, 256], BF16)
            wmk = arena.alloc([16, 512], BF16)
            wmv = arena.alloc([16, 512], BF16)
            cm = P.newctr()
            cg = P.newctr()
            ck = P.newctr()
            cv = P.newctr()
            t0 = P.op("dve", MSET(st, 0.0))
            tm = P.dma("sp", memx, mem_d.rearrange("(t p) d -> p t d", p=128), cm)
            s4 = []
            for t in range(2):
                a1 = P.op("act", ACTV(junk, memx[:, t, :], AF.Square, accum_out=st[:, t:t + 1]), [tm, t0])
                a2 = P.op("act", ACTV(st[:, 2 + t:3 + t], st[:, t:t + 1], AF.Ln, scale=1.0 / D, bias=EPS), [a1])
                a3 = P.op("act", ACTV(st[:, 4 + t:5 + t], st[:, 2 + t:3 + t], AF.Exp, scale=-0.5), [a2])
                s4.append(P.op("dve", TS(memx[:, t, :], memx[:, t, :], st[:, 4 + t:5 + t], None, ALU.mult), [a3, a1]))
            gm_free, wmk_free, wmv_free, memn_free, memnT_free = [], [], [], [], []
            ring = 0
            for l in range(nl):
                if WL0 and l > 0:
                    continue
                tg = P.dma("sp", gm, mem_g[l:l + 1, :].broadcast_to([128, D]), cg, gm_free)
                tk = P.dma("pool", wmk, w_mk[l].rearrange("(c p) n -> p c n", p=128), ck, wmk_free)
                tv = P.dma("pool", wmv, w_mv[l].rearrange("(c p) n -> p c n", p=128), cv, wmv_free)
                nts = [P.op("dve", TT(memn[:, t, :], memx[:, t, :], gm, ALU.mult), [tg, s4, memn_free])
                       for t in range(2)]
                gm_free = [nts[1]]
                evs = []
                last_tr = None
                for t in range(2):
                    for half in range(2):
                        b = ring % 8
                        ring += 1
                        for c8 in range(8):
                            c = half * 8 + c8
                            last_tr = P.op("pe", TR(PSB[b][:, c8, :], memn[:, t, c * 128:(c + 1) * 128], ident),
                                           [nts[t], bankfree[b]], inc=(c8 == 7))
                        eng = "act" if half == 0 else "dve"
                        ev = P.op(eng, CP(memnT[:, half * 8:(half + 1) * 8, t * 128:(t + 1) * 128], PSB[b]),
                                  [last_tr, memnT_free])
                        bankfree[b] = [ev]
                        evs.append(ev)
                memn_free = [last_tr]
                last_mm = None
                for j in range(4):
                    b = ring % 8
                    ring += 1
                    for c in range(16):
                        last_mm = P.op("pe", MM(PSA[b][:, 0:256], wmk[:, c, j * 128:(j + 1) * 128], memnT[:, c, :],
                                                c == 0, c == 15), [tk, evs, bankfree[b]], inc=(c == 15))
                    ev = P.op("act" if j % 2 == 0 else "dve", CP(MKT[:, l, j, :], PSA[b][:, 0:256]), [last_mm])
                    bankfree[b] = [ev]
                wmk_free = [last_mm]
                for t in range(2):
                    b = ring % 8
                    ring += 1
                    for c in range(16):
                        last_mm = P.op("pe", MM(PSA[b], memnT[:, c, t * 128:(t + 1) * 128], wmv[:, c, :],
                                                c == 0, c == 15), [tv, evs, bankfree[b]], inc=(c == 15))
                    ev = P.op("act" if t % 2 == 0 else "dve", CP(MV[:, l, t, :], PSA[b]), [last_mm])
                    bankfree[b] = [ev]
                wmv_free = [last_mm]
                memnT_free = [last_mm]
            P.barrier()
            arena.reset(persist_mark)

        def phase1_half(l, hf):
            src = x_d if (l == 0 or SRCX) else XS
            if WL0:
                l = 0
            hT = arena.alloc([16, 2048], BF16)
            wslots = [arena.alloc([16, 512], BF16) for _ in range(3)]
            pm = arena.mark()
            gpre = arena.alloc([D], F32)
            xts = [arena.alloc([D], F32) for _ in range(3)]
            hbs = [arena.alloc([D], BF16) for _ in range(2)]
            junk = arena.alloc([D], BF16)
            stats = arena.alloc([16, 4], F32)
            xt_ring = Ring(P, xts)
            hb_ring = Ring(P, hbs, ctr=False)
            cg = P.newctr()
            t0 = P.op("dve", MSET(stats, 0.0))
            tg = P.dma("sp", gpre, pre_g[l:l + 1, :].broadcast_to([128, D]), cg)
            tpi = 0
            last_ev = {"act": None, "dve": None}
            for tt in range(16):
                tok0 = hf * 2048 + tt * 128
                i = xt_ring.take()
                tx = P.dma("sp", xts[i], src[tok0:tok0 + 128, :], xt_ring.ctr[i], xt_ring.free[i])
                sq = P.op("act", ACTV(junk, xts[i], AF.Square, accum_out=stats[:, tt, 0:1]), [tx, t0])
                ln = P.op("act", ACTV(stats[:, tt, 1:2], stats[:, tt, 0:1], AF.Ln, scale=1.0 / D, bias=EPS), [sq])
                rs = P.op("act", ACTV(stats[:, tt, 2:3], stats[:, tt, 1:2], AF.Exp, scale=-0.5), [ln])
                k = hb_ring.take()
                hh = P.op("dve", STT(hbs[k], xts[i], stats[:, tt, 2:3], gpre, ALU.mult, ALU.mult),
                          [rs, tg, hb_ring.free[k]])
                xt_ring.free[i] = [hh, sq]
                lasts = []
                for half in range(2):
                    b = tpi % 4
                    tpi += 1
                    last = None
                    for c8 in range(8):
                        c = half * 8 + c8
                        last = P.op("pe", TR(PSB[b][:, c8, :], hbs[k][:, c * 128:(c + 1) * 128], ident),
                                    [hh, bankfree[b]], inc=(c8 == 7))
                    eng = "act" if half == 0 else "dve"
                    ev = P.op(eng, CP(hT[:, half * 8:(half + 1) * 8, tt * 128:(tt + 1) * 128], PSB[b]), [last])
                    bankfree[b] = [ev]
                    last_ev[eng] = ev
                    lasts.append(last)
                hb_ring.free[k] = [lasts[1]]
            hT_ready = [last_ev["act"], last_ev["dve"]]
            arena.reset(pm)

            wring = Ring(P, wslots)
            T = TmpPool(P, [arena.alloc([512], F32) for _ in range(24)])
            for idx in range(T.n):
                T.free[idx] = list(hT_ready)

            groups = []
            groups.append(dict(blocks=[[(CQ, 512, 0)]], jobs=[("ql", j, 0, j * 128) for j in range(4)]))
            groups.append(dict(blocks=[[(CKV, 256, 0), (CKPE, 64, 256), (CKPE + 32, 32, 320), (CKPE, 32, 352)]],
                               jobs=[("kv", 0, 0, 0), ("kv", 1, 0, 128), ("kpe", 0, 0, 256)]))
            for j in range(4):
                groups.append(dict(blocks=[[(CGC + 128 * j, 128, 0), (CXIN + 128 * j, 128, 128),
                                            (CGB + 128 * j, 128, 256), (CG + 1024 + 128 * j, 128, 384)]],
                                   jobs=[("gc", j, 0, 0), ("xin", j, 0, 128), ("gb", j, 0, 256), ("gcv", j, 0, 384)]))
            jobs = []
            for j in range(4):
                jobs += [("qm", j, 0, j * 128), ("gm", j, 1, j * 128)]
            groups.append(dict(blocks=[[(CQM, 512, 0)], [(CG + 1536, 512, 0)]], jobs=jobs))
            groups.append(dict(blocks=[[(CG, 512, 0)]], jobs=[("g", c, 0, c * 128) for c in range(4)]))
            groups.append(dict(blocks=[[(CG + 512, 512, 0)]], jobs=[("g", 4 + c, 0, c * 128) for c in range(4)]))

            wv = w_in[l].rearrange("(c p) n -> p c n", p=128)

            def load_group(g):
                res = []
                for pieces in g["blocks"]:
                    si = wring.take()
                    tk = None
                    for (sc, ncol, dc) in pieces:
                        tk = P.dma("pool", wslots[si][:, :, dc:dc + ncol], wv[:, :, sc:sc + ncol], wring.ctr[si],
                                   wring.free[si])
                    res.append((si, tk))
                return res

            deferred = []

            def defer(k, fn):
                deferred.append([k, fn])

            def after_job():
                nonlocal deferred
                cur = deferred
                deferred = []
                keep = []
                for item in cur:
                    item[0] -= 1
                    if item[0] <= 0:
                        item[1]()
                    else:
                        keep.append(item)
                deferred = keep + deferred

            X0, X1, X2, X3 = 4, 5, 6, 7
            mi = [0]
            state = {}

            def run_job(kind, j, slot, tok_w, col, tc, last_use):
                g0 = hf * 2048 + tc * 512
                b = mi[0] % 4
                mi[0] += 1
                bank = PSA[b]
                tokM = None
                for c in range(16):
                    tokM = P.op("pe", MM(bank, wslots[slot][:, c, col:col + 128], hT[:, c, tc * 512:(tc + 1) * 512],
                                         c == 0, c == 15),
                                [tok_w, hT_ready, bankfree[b]] if c == 0 else (), inc=(c == 15))
                if last_use:
                    wring.free[slot] = wring.free[slot] + [tokM]
                after_job()

                if kind in ("ql", "kv"):
                    nchunk = 4 if kind == "ql" else 2
                    a, wa = T.get()
                    tq1 = P.op("act", ACTV(T.f32(a), bank, AF.Copy), [tokM, wa])
                    s_, ws = T.get()
                    tq2 = P.op("act", ACTV(T.bf16(s_), bank, AF.Square), [tokM, ws])
                    bankfree[b] = [tq1, tq2]
                    lst = state.setdefault((kind, tc), [])
                    lst.append((a, tq1, s_, tq2))
                    if j == nchunk - 1:
                        chunks = list(lst)
                        X = X0 if kind == "ql" else X1
                        gvec = gq if kind == "ql" else gkv
                        dstv = QNv if kind == "ql" else KVNv
                        nf = 512.0 if kind == "ql" else 256.0

                        def stage2():
                            tk = None
                            for idx, (a_, tq1_, s2, tq2_) in enumerate(chunks):
                                tk = P.op("pe", MM(PSA[X], ones, T.bf16(s2), idx == 0, idx == nchunk - 1),
                                          [tq2_, bankfree[X]], inc=(idx == nchunk - 1))
                            for (_, _, s2, _) in chunks:
                                T.used(s2, tk)
                            r1, w1 = T.get()
                            t_ln = P.op("act", ACTV(T.f32(r1), PSA[X], AF.Ln, scale=1.0 / nf, bias=EPS), [tk, w1])
                            bankfree[X] = [t_ln]
                            r2, w2 = T.get()
                            t_rs = P.op("act", ACTV(T.f32(r2), T.f32(r1), AF.Exp, scale=-0.5), [t_ln, w2])
                            T.used(r1, t_rs)
                            for idx, (a_, tq1_, s2, tq2_) in enumerate(chunks):
                                o, wo_ = T.get()
                                t_n = P.op("dve", STT(T.bf16(o), T.f32(a_), gvec[:, l, idx:idx + 1], T.f32(r2),
                                                      ALU.mult, ALU.mult), [tq1_, t_rs, wo_])
                                T.used(a_, t_n)
                                T.used(r2, t_n)
                                st_ = P.dma("sp", dstv[:, idx, g0:g0 + 512], T.bf16(o), T.ctr[o], [t_n])
                                T.used(o, st_)
                        defer(1, stage2)
                elif kind == "kpe":
                    pr, w = T.get()
                    t1 = P.op("dve", TT(T.bf16(pr), bank, TQ[:, g0:g0 + 512], ALU.mult), [tokM, w])
                    bankfree[b] = [t1]

                    def stage2():
                        tk = P.op("pe", MM(PSA[X2], dmat, T.bf16(pr), True, True), [t1, bankfree[X2]])
                        T.used(pr, tk)
                        o, wo_ = T.get()
                        t2 = P.op("act", ACTV(T.bf16(o), PSA[X2], AF.Copy), [tk, wo_])
                        bankfree[X2] = [t2]
                        st_ = P.dma("sp", KPE[:, g0:g0 + 512], T.bf16(o), T.ctr[o], [t2])
                        T.used(o, st_)
                    defer(1, stage2)
                elif kind in ("gc", "gb"):
                    a, wa = T.get()
                    t = P.op("act", ACTV(T.f32(a), bank, AF.Copy), [tokM, wa])
                    bankfree[b] = [t]
                    state[(kind, j, tc)] = (a, t)
                elif kind == "xin":
                    a, tg_ = state.pop(("gc", j, tc))
                    u, wu = T.get()
                    t = P.op("dve", TT(T.f32(u), bank, T.f32(a), ALU.mult), [tokM, tg_, wu])
                    bankfree[b] = [t]
                    T.used(a, t)
                    st_ = P.dma("sp", Uv[:, j, g0:g0 + 512], T.f32(u), T.ctr[u], [t])
                    T.used(u, st_)
                elif kind == "gcv":
                    a, tb_ = state.pop(("gb", j, tc))
                    s_, ws = T.get()
                    t1 = P.op("act", ACTV(T.f32(s_), bank, AF.Silu), [tokM, ws])
                    bankfree[b] = [t1]
                    o, wo_ = T.get()
                    t2 = P.op("dve", TT(T.bf16(o), T.f32(a), T.f32(s_), ALU.mult), [tb_, t1, wo_])
                    T.used(a, t2)
                    T.used(s_, t2)
                    st_ = P.dma("sp", GBSGv[:, j, g0:g0 + 512], T.bf16(o), T.ctr[o], [t2])
                    T.used(o, st_)
                elif kind == "qm":
                    q, wq_ = T.get()
                    t1 = P.op("act", ACTV(T.bf16(q), bank, AF.Copy), [tokM, wq_])
                    bankfree[b] = [t1]

                    def stage2():
                        XS = [X0, X1]
                        ts = []
                        for mt in range(2):
                            ts.append(P.op("pe", MM(PSA[XS[mt]], MKT[:, l, j, mt * 128:(mt + 1) * 128], T.bf16(q),
                                                    True, True), [t1, bankfree[XS[mt]]]))
                        T.used(q, ts[1])
                        pms = []
                        for mt in range(2):
                            p_, wp = T.get()
                            te = P.op("act", ACTV(T.bf16(p_), PSA[XS[mt]], AF.Exp, scale=128.0 ** -0.5), [ts[mt], wp])
                            bankfree[XS[mt]] = [te]
                            pms.append((p_, te))

                        def stage3():
                            tk_o = tk_l = None
                            for mt in range(2):
                                tk_o = P.op("pe", MM(PSA[X2], MV[:, l, mt, j * 128:(j + 1) * 128], T.bf16(pms[mt][0]),
                                                     mt == 0, mt == 1), [pms[mt][1], bankfree[X2]], inc=(mt == 1))
                            for mt in range(2):
                                tk_l = P.op("pe", MM(PSA[X3], ones, T.bf16(pms[mt][0]), mt == 0, mt == 1),
                                            [bankfree[X3]], inc=(mt == 1))
                            for mt in range(2):
                                T.used(pms[mt][0], tk_l)
                            lc, wl = T.get()
                            t_lc = P.op("act", ACTV(T.f32(lc), PSA[X3], AF.Copy), [tk_l, wl])
                            bankfree[X3] = [t_lc]
                            rcp, wr = T.get()
                            t_rc = P.op("dve", RCP(T.f32(rcp), T.f32(lc)), [t_lc, wr])
                            T.used(lc, t_rc)
                            tm_, wt = T.get()
                            t_t = P.op("dve", TT(T.f32(tm_), PSA[X2], T.f32(rcp), ALU.mult), [tk_o, t_rc, wt])
                            bankfree[X2] = [t_t]
                            T.used(rcp, t_t)
                            s_, ts1 = state.pop(("sgm", j, tc))
                            o, wo_ = T.get()
                            t_y = P.op("dve", TT(T.bf16(o), T.f32(tm_), T.f32(s_), ALU.mult), [t_t, ts1, wo_])
                            T.used(tm_, t_y)
                            T.used(s_, t_y)
                            st_ = P.dma("sp", YTv[:, 12 + j, g0:g0 + 512], T.bf16(o), T.ctr[o], [t_y])
                            T.used(o, st_)
                        defer(1, stage3)
                    defer(1, stage2)
                elif kind == "gm":
                    s_, ws = T.get()
                    t1 = P.op("act", ACTV(T.f32(s_), bank, AF.Silu), [tokM, ws])
                    bankfree[b] = [t1]
                    state[("sgm", j, tc)] = (s_, t1)
                elif kind == "g":
                    o, wo_ = T.get()
                    t = P.op("act", ACTV(T.bf16(o), bank, AF.Silu), [tokM, wo_])
                    bankfree[b] = [t]
                    st_ = P.dma("sp", SGv[:, j, g0:g0 + 512], T.bf16(o), T.ctr[o], [t])
                    T.used(o, st_)

            loaded = load_group(groups[0])
            for gi, g in enumerate(groups):
                cur = loaded
                if gi + 1 < len(groups):
                    loaded = load_group(groups[gi + 1])
                njobs = len(g["jobs"])
                for tc in range(4):
                    for ji, (kind, j, bi, col) in enumerate(g["jobs"]):
                        slot, tok_w = cur[bi]
                        last_use = (tc == 3) and all(bj != bi for (_, _, bj, _) in g["jobs"][ji + 1:])
                        run_job(kind, j, slot, tok_w, col, tc, last_use)
            while deferred:
                after_job()
            P.barrier()
            arena.reset(persist_mark)

        def phase_conv(l):
            if WL0:
                l = 0
            ubs = [arena.alloc([S + 2], F32) for _ in range(2)]
            gss = [arena.alloc([S], BF16) for _ in range(2)]
            acc = arena.alloc([S], F32)
            ycs = [arena.alloc([S], BF16) for _ in range(2)]
            ub_r = Ring(P, ubs)
            gs_r = Ring(P, gss)
            yc_r = Ring(P, ycs)
            tz = [P.op("dve", MSET(ubs[i][:, 0:1], 0.0)) for i in range(2)]
            tz += [P.op("dve", MSET(ubs[i][:, S + 1:S + 2], 0.0)) for i in range(2)]
            acc_free = []
            for j in range(4):
                i = ub_r.take()
                tu = P.dma("sp", ubs[i][:, 1:S + 1], Uv[:, j, :], ub_r.ctr[i], ub_r.free[i])
                k = gs_r.take()
                tg = P.dma("sp", gss[k], GBSGv[:, j, :], gs_r.ctr[k], gs_r.free[k])
                a1 = P.op("dve", TS(acc, ubs[i][:, 1:S + 1], cw[:, l, 1, j:j + 1], None, ALU.mult), [tu, tz, acc_free])
                a2 = P.op("dve", STT(acc, ubs[i][:, 0:S], cw[:, l, 0, j:j + 1], acc, ALU.mult, ALU.add), [a1])
                a3 = P.op("dve", STT(acc, ubs[i][:, 2:S + 2], cw[:, l, 2, j:j + 1], acc, ALU.mult, ALU.add), [a2])
                ub_r.free[i] = [a3]
                m_ = yc_r.take()
                a4 = P.op("dve", TT(ycs[m_], acc, gss[k], ALU.mult), [a3, tg, yc_r.free[m_]])
                acc_free = [a4]
                gs_r.free[k] = [a4]
                st_ = P.dma("sp", YTv[:, 8 + j, :], ycs[m_], yc_r.ctr[m_], [a4])
                yc_r.free[m_] = [st_]
            P.barrier()
            arena.reset(persist_mark)

        def phase2(l):
            if WL0:
                l = 0
            QNs = arena.alloc([4, S], BF16)
            KVNs = arena.alloc([2, S], BF16)
            KPEs = arena.alloc([S], BF16)
            wq = arena.alloc([4, 1536], BF16)
            wqp = arena.alloc([4, 8, 128], BF16)
            wkv = arena.alloc([2, 2048], BF16)
            KnT = arena.alloc([S], BF16)
            V = arena.alloc([32, 128], BF16)
            QnT = arena.alloc([S], BF16)
            QpT = arena.alloc([S], BF16)
            Pb = [arena.alloc([512], BF16) for _ in range(4)]
            sgbs = [arena.alloc([512], BF16) for _ in range(2)]
            Lc = [arena.alloc([512], F32) for _ in range(2)]
            rec = [arena.alloc([512], F32) for _ in range(2)]
            tmp = [arena.alloc([512], F32) for _ in range(2)]
            ybs = [arena.alloc([512], BF16) for _ in range(2)]
            sg_r = Ring(P, sgbs)
            yb_r = Ring(P, ybs)
            c_in = P.newctr()
            c_w = P.newctr()
            P.dma("sp", QNs, QNv, c_in)
            P.dma("sp", KVNs, KVNv, c_in)
            P.dma("sp", KPEs, KPE, c_in)
            t_in = (c_in, c_in.count)
            P.dma("pool", wq, w_uq[l].rearrange("(c p) n -> p c n", p=128), c_w)
            wv4 = w_uq[l].rearrange("(c p) (h j) -> p c h j", p=128, j=192)
            for c in range(4):
                P.dma("pool", wqp[:, c, :, 0:64], wv4[:, c, :, 128:192], c_w)
                P.dma("pool", wqp[:, c, :, 64:96], wv4[:, c, :, 160:192], c_w)
                P.dma("pool", wqp[:, c, :, 96:128], wv4[:, c, :, 128:160], c_w)
            P.dma("pool", wkv, w_ukv[l].rearrange("(c p) n -> p c n", p=128), c_w)
            t_w = (c_w, c_w.count)

            SB = [0, 1, 2]
            PRB = [0, 1, 2, 3]
            OB = [4, 5]
            LB = [6, 7]
            scale = 192.0 ** -0.5
            head_done = []
            Lc_free = [[], []]
            rec_free = [[], []]
            tmp_free = [[], []]
            P_free = [[] for _ in range(4)]
            pri = 0
            evi = 0
            for h in range(HEADS):
                last = {"act": None, "dve": None}

                def evac(fn_builder, tk, force=None):
                    nonlocal evi
                    eng = force or ("act" if evi % 2 == 0 else "dve")
                    evi += 1
                    ev = P.op(eng, fn_builder, [tk, head_done])
                    last[eng] = ev
                    return ev

                for tc in range(8):
                    b = PRB[pri % 4]
                    pri += 1
                    tk = None
                    for c in range(2):
                        tk = P.op("pe", MM(PSA[b], wkv[:, c, h * 256:h * 256 + 128], KVNs[:, c, tc * 512:(tc + 1) * 512],
                                           c == 0, c == 1), [t_w, t_in, bankfree[b]], inc=(c == 1))
                    bankfree[b] = [evac(CP(KnT[:, tc * 512:(tc + 1) * 512], PSA[b]), tk)]
                for g in range(8):
                    b = PRB[pri % 4]
                    pri += 1
                    tk = None
                    for t4 in range(4):
                        tt = g * 4 + t4
                        for c in range(2):
                            tk = P.op("pe", MM(PSA[b][:, t4 * 128:(t4 + 1) * 128], KVNs[:, c, tt * 128:(tt + 1) * 128],
                                               wkv[:, c, h * 256 + 128:h * 256 + 256], c == 0, c == 1),
                                      [t_w, t_in, bankfree[b]], inc=(t4 == 3 and c == 1))
                    bankfree[b] = [evac(CP(V[:, g * 4:(g + 1) * 4, :],
                                           PSA[b].rearrange("p (a b) -> p a b", b=128)), tk)]
                for tc in range(8):
                    b = PRB[pri % 4]
                    pri += 1
                    tk = None
                    for c in range(4):
                        tk = P.op("pe", MM(PSA[b], wq[:, c, h * 192:h * 192 + 128], QNs[:, c, tc * 512:(tc + 1) * 512],
                                           c == 0, c == 3), [t_w, t_in, bankfree[b]], inc=(c == 3))
                    bankfree[b] = [evac(CP(QnT[:, tc * 512:(tc + 1) * 512], PSA[b]), tk)]
                for tc in range(8):
                    b = PRB[pri % 4]
                    pri += 1
                    tk = None
                    for c in range(4):
                        tk = P.op("pe", MM(PSA[b], wqp[:, c, h, :], QNs[:, c, tc * 512:(tc + 1) * 512],
                                           c == 0, c == 3), [t_w, t_in, bankfree[b]], inc=(c == 3))
                    bankfree[b] = [evac(TT(QpT[:, tc * 512:(tc + 1) * 512], PSA[b], TQ[:, tc * 512:(tc + 1) * 512],
                                           ALU.mult), tk, force="dve")]
                prep_done = [last["act"], last["dve"]]

                NIT = 256
                LA = 2
                tokQK = [None] * NIT
                tokE = [None] * NIT
                tokPV = [None] * NIT
                sg_tok = {}
                for i in range(NIT + LA):
                    if i < NIT:
                        qc, kt = divmod(i, 32)
                        if kt == 0:
                            si = sg_r.take()
                            sg_tok[qc] = (si, P.dma("sp", sgbs[si], SGv[:, h, qc * 512:(qc + 1) * 512], sg_r.ctr[si],
                                                    sg_r.free[si]))
                        sb = SB[i % 3]
                        P.op("pe", MM(PSA[sb], KnT[:, kt * 128:(kt + 1) * 128], QnT[:, qc * 512:(qc + 1) * 512],
                                      True, False), [prep_done, bankfree[sb]], inc=False)
                        tokQK[i] = P.op("pe", MM(PSA[sb], KPEs[:, kt * 128:(kt + 1) * 128],
                                                 QpT[:, qc * 512:(qc + 1) * 512], False, True))
                        pi = i % 4
                        tokE[i] = P.op("act", ACTV(Pb[pi], PSA[sb], AF.Exp, scale=scale), [tokQK[i], P_free[pi]])
                        bankfree[sb] = [tokE[i]]
                    jn = i - LA
                    if jn >= 0:
                        qc, kt = divmod(jn, 32)
                        ob = OB[qc % 2]
                        lb = LB[qc % 2]
                        pi = jn % 4
                        P.op("pe", MM(PSA[ob], V[:, kt, :], Pb[pi], kt == 0, kt == 31),
                             [tokE[jn], bankfree[ob] if kt == 0 else None], inc=False)
                        tokPV[jn] = P.op("pe", MM(PSA[lb], ones, Pb[pi], kt == 0, kt == 31),
                                         [bankfree[lb] if kt == 0 else None])
                        P_free[pi] = [tokPV[jn]]
                        if kt == 31:
                            k = qc % 2
                            t_lc = P.op("act", ACTV(Lc[k], PSA[lb], AF.Copy), [tokPV[jn], Lc_free[k]])
                            t_rc = P.op("dve", RCP(rec[k], Lc[k]), [t_lc, rec_free[k]])
                            t_t = P.op("dve", TT(tmp[k], PSA[ob], rec[k], ALU.mult), [tokPV[jn], t_rc, tmp_free[k]])
                            bankfree[ob] = [t_t]
                            bankfree[lb] = [t_lc]
                            si, t_sg = sg_tok.pop(qc)
                            yi = yb_r.take()
                            t_y = P.op("dve", TT(ybs[yi], tmp[k], sgbs[si], ALU.mult), [t_t, t_sg, yb_r.free[yi]])
                            st_ = P.dma("sp", YTv[:, h, qc * 512:(qc + 1) * 512], ybs[yi], yb_r.ctr[yi], [t_y])
                            yb_r.free[yi] = [st_]
                            sg_r.free[si] = [t_y]
                            Lc_free[k] = [t_rc]
                            rec_free[k] = [t_t]
                            tmp_free[k] = [t_y]
                head_done = [tokPV[NIT - 1]]
            P.barrier()
            arena.reset(persist_mark)

        def phase3(l):
            src = x_d if (l == 0 or SRCX) else XS
            dst = out_d if l == nl - 1 else XS
            if WL0:
                l = 0
            wo = arena.alloc([16, 2048], BF16)
            gpost = arena.alloc([D], F32)
            ytss = [arena.alloc([16, 512], BF16) for _ in range(2)]
            xts = [arena.alloc([D], F32) for _ in range(3)]
            ots = [arena.alloc([D], F32) for _ in range(2)]
            junk = arena.alloc([512], BF16)
            stats = arena.alloc([32, 8], F32)
            yt_r = Ring(P, ytss)
            xt_r = Ring(P, xts)
            ot_r = Ring(P, ots, ctr=False)
            cwo = P.newctr()
            cg = P.newctr()
            wov = w_o[l].rearrange("(c p) n -> p c n", p=128)
            for n in range(4):
                P.dma("pool", wo[:, :, n * 512:(n + 1) * 512], wov[:, :, n * 512:(n + 1) * 512], cwo)
            t_wo = (cwo, cwo.count)
            t_g = P.dma("sp", gpost, post_g[l:l + 1, :].broadcast_to([128, D]), cg)
            t0 = P.op("dve", MSET(stats, 0.0))
            i_y = None
            t_y = None
            for tt in range(32):
                if tt % 4 == 0:
                    i_y = yt_r.take()
                    t_y = P.dma("sp", ytss[i_y], YTv[:, :, tt * 128:tt * 128 + 512], yt_r.ctr[i_y], yt_r.free[i_y])
                i_x = xt_r.take()
                t_x = P.dma("sp", xts[i_x], src[tt * 128:(tt + 1) * 128, :], xt_r.ctr[i_x], xt_r.free[i_x])
                banks = [(tt % 2) * 4 + n for n in range(4)]
                tk = [None] * 4
                for n in range(4):
                    for c in range(16):
                        tk[n] = P.op("pe", MM(PSA[banks[n]], ytss[i_y][:, c, (tt % 4) * 128:(tt % 4) * 128 + 128],
                                              wo[:, c, n * 512:(n + 1) * 512], c == 0, c == 15),
                                     [t_y, t_wo, bankfree[banks[n]]] if c == 0 else (), inc=(c == 15))
                if tt % 4 == 3:
                    yt_r.free[i_y] = [tk[3]]
                t_sq = [P.op("act", ACTV(junk, PSA[banks[n]], AF.Square, accum_out=stats[:, tt, n:n + 1]), [tk[n], t0])
                        for n in range(4)]
                t_a = P.op("dve", TT(stats[:, tt, 4:6], stats[:, tt, 0:2], stats[:, tt, 2:4], ALU.add), [t_sq[3]])
                t_b = P.op("dve", TT(stats[:, tt, 6:7], stats[:, tt, 4:5], stats[:, tt, 5:6], ALU.add), [t_a])
                t_ln = P.op("act", ACTV(stats[:, tt, 7:8], stats[:, tt, 6:7], AF.Ln, scale=1.0 / D, bias=EPS), [t_b])
                t_rs = P.op("act", ACTV(stats[:, tt, 4:5], stats[:, tt, 7:8], AF.Exp, scale=-0.5), [t_ln])
                i_o = ot_r.take()
                t_o = None
                for n in range(4):
                    t_o = P.op("dve", STT(ots[i_o][:, n * 512:(n + 1) * 512], PSA[banks[n]], stats[:, tt, 4:5],
                                          gpost[:, n * 512:(n + 1) * 512], ALU.mult, ALU.mult),
                               [t_rs, t_g, ot_r.free[i_o]])
                    bankfree[banks[n]] = [t_o]
                t_add = P.op("dve", TT(xts[i_x], ots[i_o], xts[i_x], ALU.add), [t_o, t_x])
                ot_r.free[i_o] = [t_add]
                st_ = P.dma("sp", dst[tt * 128:(tt + 1) * 128, :], xts[i_x], xt_r.ctr[i_x], [t_add])
                xt_r.free[i_x] = [st_]
            P.barrier()
            arena.reset(persist_mark)

        phase0()
        for l in range(nl):
            if l > 0 and NEWCTR:
                P.new_engine_counters()
            for hf in range(2):
                phase1_half(l, hf)
            phase_conv(l)
            phase2(l)
            phase3(l)

        block = es.enter_context(nc.Block())

        @block.tensor
        def _(e):
            P.emit("pe", e)

        @block.scalar
        def _(e):
            P.emit("act", e)

        @block.vector
        def _(e):
            P.emit("dve", e)

        @block.gpsimd
        def _(e):
            P.emit("pool", e)

        @block.sync
        def _(e):
            P.emit("sp", e)

    return nc


def rope_consts():
    inv = (1.0 / (np.float32(10000.0) ** (np.arange(0, 64, 2, dtype=np.float32) / np.float32(64)))).astype(np.float32)
    rc = np.zeros((128, 2), np.float32)
    for r in range(128):
        rc[r, 0] = inv[r % 32]
    rc[0:64, 1] = np.float32(math.pi / 2)
    rc[64:96, 1] = np.float32(math.pi)
    rc[96:128, 1] = 0.0
    return rc


WNAMES = ["pre_norm_g", "w_in", "q_norm_g", "w_uq", "kv_norm_g", "w_ukv", "conv_w", "mem_norm_g", "w_mk", "w_mv",
          "w_o", "post_norm_g"]


def make_in_maps(inputs, cores):
    f = lambda a: np.ascontiguousarray(np.asarray(a, dtype=np.float32))
    wts = {k: f(inputs[k]) for k in WNAMES}
    rc = rope_consts()
    maps = []
    for b in cores:
        m = dict(wts)
        m["x"] = f(inputs["x"][b])
        m["mem"] = f(inputs["mem"][b])
        m["pos"] = np.ascontiguousarray(np.asarray(inputs["positions"][b], dtype=np.int32)[None, :])
        m["ropec"] = rc
        maps.append(m)
    return maps


def kernel(**inputs):
    if FUSED:
        nc = build_program(DEPTH)
        in_maps = make_in_maps(inputs, list(range(NCORES)))
        res = run_bass_kernel_spmd(nc, in_maps, core_ids=list(range(NCORES)))
        return np.stack([np.asarray(res.results[b]["out"], dtype=np.float32) for b in range(NCORES)], axis=0)
    nc = build_program(1, wdepth=1)
    x = np.asarray(inputs["x"], dtype=np.float32)
    for l in range(DEPTH):
        cur = dict(inputs)
        cur["x"] = x
        for k in WNAMES:
            cur[k] = np.asarray(inputs[k])[l:l + 1]
        in_maps = make_in_maps(cur, list(range(NCORES)))
        res = run_bass_kernel_spmd(nc, in_maps, core_ids=list(range(NCORES)))
        x = np.stack([np.asarray(res.results[b]["out"], dtype=np.float32) for b in range(NCORES)], axis=0)
    return x
```
